# Optimizing a Trainium2 kernel written in Bass

```python
import jax, jax.numpy as jnp
from jax import lax
import numpy as np

D_MODEL = 2048
BATCH = 4
SEQ = 4096
DEPTH = 1

CTX_LEN = 256
GRID_W = 64

MIX_A = 1024
MIX_B = 1024
SGU_GROUPS = 4
SGU_CHUNK = 128
SGU_GW = MIX_A // SGU_GROUPS
GLA_HEADS = 4
GLA_DK = MIX_B // 2 // GLA_HEADS
GLA_DV = MIX_B // GLA_HEADS
GLA_KDIM = GLA_HEADS * GLA_DK
GLA_LOWRANK = 16
GLA_NORMALIZER = 16.0
GLA_CHUNK = 64
D_FF = 5632
CONV_W = 3
EPS = 1e-6
N_IN = 2 * MIX_A + 2 * GLA_KDIM + 2 * MIX_B + 2 * GLA_LOWRANK
N_MOD = 6

kernel_name = "hybrid_sgu_gla_convglu_prefix_dit_layer"


def rmsnorm(x, g):
    xf = x.astype(jnp.float32)
    y = xf * lax.rsqrt(jnp.mean(xf * xf, axis=-1, keepdims=True) + EPS)
    return (y * g.astype(jnp.float32)).astype(x.dtype)


def layernorm(x, g, b):
    xf = x.astype(jnp.float32)
    mu = jnp.mean(xf, axis=-1, keepdims=True)
    var = jnp.mean(jnp.square(xf - mu), axis=-1, keepdims=True)
    y = (xf - mu) * lax.rsqrt(var + EPS)
    return (y * g.astype(jnp.float32) + b.astype(jnp.float32)).astype(x.dtype)


def flip_t(t):
    return jnp.flip(t, axis=1)


def spatial_gating(u, v, ln_g, ln_b, w_s, b_s):
    B, L, _ = v.shape
    n = L // SGU_CHUNK
    v = v.reshape(B, n, SGU_CHUNK, SGU_GROUPS, SGU_GW)
    v = layernorm(v, ln_g.reshape(SGU_GROUPS, SGU_GW), ln_b.reshape(SGU_GROUPS, SGU_GW))
    s = jnp.einsum('gpq,bnqgc->bnpgc', w_s, v) + b_s.T[:, :, None]
    return u * s.reshape(B, L, MIX_A)


def gla_log_decay(lr, w, b):
    B, L, _ = lr.shape
    z = (lr @ w + b).astype(jnp.float32)
    return (jax.nn.log_sigmoid(z) / GLA_NORMALIZER).reshape(B, L, GLA_HEADS, GLA_DK)


def gla_chunked(q, k, v, g, s0):
    B, L, H, DK = q.shape
    n = L // GLA_CHUNK
    f = lambda t: t.astype(jnp.float32).reshape(B, n, GLA_CHUNK, H, t.shape[-1])
    q, k, v, g = f(q), f(k), f(v), f(g)
    b = jnp.cumsum(g, axis=2)
    b_last = b[:, :, -1:]
    q_e = q * jnp.exp(b)
    k_e = k * jnp.exp(-b)
    k_d = k * jnp.exp(b_last - b)
    mask = jnp.tril(jnp.ones((GLA_CHUNK, GLA_CHUNK), dtype=bool))
    att = jnp.where(mask, jnp.einsum('bnthd,bnshd->bnhts', q_e, k_e), 0.0)
    o_intra = jnp.einsum('bnhts,bnshv->bnthv', att, v)

    def step(S, inp):
        qe_c, kd_c, v_c, dec_c = inp
        o_c = jnp.einsum('bthd,bhdv->bthv', qe_c, S)
        S = dec_c[..., None] * S + jnp.einsum('bthd,bthv->bhdv', kd_c, v_c)
        return S, o_c

    xs = (jnp.moveaxis(q_e, 1, 0), jnp.moveaxis(k_d, 1, 0), jnp.moveaxis(v, 1, 0),
          jnp.moveaxis(jnp.exp(b_last[:, :, 0]), 1, 0))
    s_fin, o_inter = lax.scan(step, s0.astype(jnp.float32), xs)
    o = o_intra + jnp.moveaxis(o_inter, 0, 1)
    return o.reshape(B, L, H, v.shape[-1]), s_fin


def mix_stream(a, s0_f, s0_b, w_in, sgu_ln_g, sgu_ln_b, sgu_w, sgu_b,
               gla_gate_w_f, gla_gate_b_f, gla_gate_w_b, gla_gate_b_b, gla_norm_g, w_out, with_output):
    B, L, _ = a.shape
    p = a @ w_in
    cuts = np.cumsum([MIX_A, MIX_A, GLA_KDIM, GLA_KDIM, MIX_B, MIX_B, GLA_LOWRANK]).tolist()
    u, v_s, q, k, v, r, lr_f, lr_b = jnp.split(p, cuts, axis=-1)
    q = q.reshape(B, L, GLA_HEADS, GLA_DK) * (GLA_DK ** -0.5)
    k = k.reshape(B, L, GLA_HEADS, GLA_DK)
    v = v.reshape(B, L, GLA_HEADS, GLA_DV)
    g_f = gla_log_decay(lr_f, gla_gate_w_f, gla_gate_b_f)
    g_b = gla_log_decay(lr_b, gla_gate_w_b, gla_gate_b_b)
    o_f, s_f = gla_chunked(q, k, v, g_f, s0_f)
    o_b_rev, s_b = gla_chunked(flip_t(q), flip_t(k), flip_t(v), flip_t(g_b), s0_b)
    if not with_output:
        return None, s_f, s_b
    o = o_f + flip_t(o_b_rev)
    o = o * lax.rsqrt(jnp.mean(o * o, axis=-1, keepdims=True) + EPS) * gla_norm_g.astype(jnp.float32)
    o = o * jax.nn.silu(r.astype(jnp.float32)).reshape(B, L, GLA_HEADS, GLA_DV)
    y_b = o.reshape(B, L, MIX_B).astype(a.dtype)
    y_a = spatial_gating(jax.nn.gelu(u, approximate=True), jax.nn.gelu(v_s, approximate=True),
                         sgu_ln_g, sgu_ln_b, sgu_w, sgu_b)
    y = jnp.concatenate([y_a, y_b], axis=-1) @ w_out
    return y, s_f, s_b


def conv_glu_ffn(h, w_up, conv_w, conv_b, w_down, rows):
    B, L, _ = h.shape
    a, val = jnp.split(h @ w_up, 2, axis=-1)
    a = a.reshape(B, rows, L // rows, D_FF)
    a = lax.conv_general_dilated(a, conv_w[:, :, None, :].astype(a.dtype), (1, 1), 'SAME',
                                 dimension_numbers=('NHWC', 'HWIO', 'NHWC'),
                                 feature_group_count=D_FF) + conv_b
    a = a.reshape(B, L, D_FF)
    return (jax.nn.gelu(a, approximate=True) * val) @ w_down


def setup_inputs(seed: int = 0) -> dict:
    key = jax.random.key(seed)
    ks = jax.random.split(key, 32)
    nrm = lambda k, shape, s: jax.random.normal(k, shape, jnp.float32) * s
    gain = lambda k, shape: 1.0 + 0.05 * jax.random.normal(k, shape, jnp.float32)
    D = D_MODEL
    return {
        "x": nrm(ks[0], (BATCH, SEQ, D), 1.0),
        "c": nrm(ks[1], (BATCH, D), 1.0),
        "ctx": nrm(ks[2], (BATCH, CTX_LEN, D), 1.0),
        "c_ctx": nrm(ks[3], (D,), 1.0),
        "ada_w": nrm(ks[4], (DEPTH, D, N_MOD * D), D ** -0.5),
        "ada_b": nrm(ks[5], (DEPTH, N_MOD * D), 0.02),
        "pre_mix_g": gain(ks[6], (DEPTH, D)),
        "post_mix_g": gain(ks[7], (DEPTH, D)),
        "pre_ffn_g": gain(ks[8], (DEPTH, D)),
        "post_ffn_g": gain(ks[9], (DEPTH, D)),
        "w_in": nrm(ks[10], (DEPTH, D, N_IN), D ** -0.5),
        "sgu_ln_g": gain(ks[11], (DEPTH, MIX_A)),
        "sgu_ln_b": nrm(ks[12], (DEPTH, MIX_A), 0.02),
        "sgu_w": nrm(ks[13], (DEPTH, SGU_GROUPS, SGU_CHUNK, SGU_CHUNK), SGU_CHUNK ** -0.5),
        "sgu_b": gain(ks[14], (DEPTH, SGU_GROUPS, SGU_CHUNK)),
        "gla_gate_w_f": nrm(ks[15], (DEPTH, GLA_LOWRANK, GLA_KDIM), GLA_LOWRANK ** -0.5),
        "gla_gate_b_f": nrm(ks[16], (DEPTH, GLA_KDIM), 0.1),
        "gla_gate_w_b": nrm(ks[17], (DEPTH, GLA_LOWRANK, GLA_KDIM), GLA_LOWRANK ** -0.5),
        "gla_gate_b_b": nrm(ks[18], (DEPTH, GLA_KDIM), 0.1),
        "gla_norm_g": gain(ks[19], (DEPTH, GLA_DV)),
        "w_out": nrm(ks[20], (DEPTH, MIX_A + MIX_B, D), (MIX_A + MIX_B) ** -0.5),
        "ffn_w_up": nrm(ks[21], (DEPTH, D, 2 * D_FF), D ** -0.5),
        "ffn_conv_w": nrm(ks[22], (DEPTH, CONV_W, CONV_W, D_FF), 1.0 / CONV_W),
        "ffn_conv_b": nrm(ks[23], (DEPTH, D_FF), 0.02),
        "ffn_w_down": nrm(ks[24], (DEPTH, D_FF, D), D_FF ** -0.5),
    }


def reference(x, c, ctx, c_ctx, ada_w, ada_b, pre_mix_g, post_mix_g, pre_ffn_g, post_ffn_g,
              w_in, sgu_ln_g, sgu_ln_b, sgu_w, sgu_b, gla_gate_w_f, gla_gate_b_f,
              gla_gate_w_b, gla_gate_b_b, gla_norm_g, w_out, ffn_w_up, ffn_conv_w, ffn_conv_b, ffn_w_down):
    B, L, _ = x.shape
    rows = L // GRID_W
    s_zero = jnp.zeros((B, GLA_HEADS, GLA_DK, GLA_DV), jnp.float32)
    h, hc = x, ctx
    for i in range(DEPTH):
        ctx_out = i < DEPTH - 1
        mod = jax.nn.silu(c) @ ada_w[i] + ada_b[i]
        sh_m, sc_m, gt_m, sh_f, sc_f, gt_f = jnp.split(mod[:, None, :], N_MOD, axis=-1)
        mod_c = jax.nn.silu(c_ctx) @ ada_w[i] + ada_b[i]
        csh_m, csc_m, cgt_m, csh_f, csc_f, cgt_f = jnp.split(mod_c, N_MOD, axis=-1)
        mix_w = (w_in[i], sgu_ln_g[i], sgu_ln_b[i], sgu_w[i], sgu_b[i], gla_gate_w_f[i], gla_gate_b_f[i],
                 gla_gate_w_b[i], gla_gate_b_b[i], gla_norm_g[i], w_out[i])
        a_ctx = rmsnorm(hc, pre_mix_g[i]) * (1.0 + csc_m) + csh_m
        y_ctx, s_f, s_b = mix_stream(a_ctx, s_zero, s_zero, *mix_w, ctx_out)
        a_lat = rmsnorm(h, pre_mix_g[i]) * (1.0 + sc_m) + sh_m
        y_lat, _, _ = mix_stream(a_lat, s_f, s_b, *mix_w, True)
        h = h + gt_m * rmsnorm(y_lat, post_mix_g[i])
        f = rmsnorm(h, pre_ffn_g[i]) * (1.0 + sc_f) + sh_f
        h = h + gt_f * rmsnorm(conv_glu_ffn(f, ffn_w_up[i], ffn_conv_w[i], ffn_conv_b[i], ffn_w_down[i], rows),
                               post_ffn_g[i])
        if ctx_out:
            hc = hc + cgt_m * rmsnorm(y_ctx, post_mix_g[i])
            fc = rmsnorm(hc, pre_ffn_g[i]) * (1.0 + csc_f) + csh_f
            hc = hc + cgt_f * rmsnorm(conv_glu_ffn(fc, ffn_w_up[i], ffn_conv_w[i], ffn_conv_b[i], ffn_w_down[i], 1),
                                      post_ffn_g[i])
    return h
```

```python
from contextlib import ExitStack
import numpy as np
import ml_dtypes
import concourse.bass as bass
import concourse.mybir as mybir
from concourse.bass_utils import run_bass_kernel_spmd

F32 = mybir.dt.float32
BF16 = mybir.dt.bfloat16
AF = mybir.ActivationFunctionType
ALU = mybir.AluOpType
AX = mybir.AxisListType

D = 2048
NT_OWN = 17
NT_ALL = 32
N_IN = 5152
DFF = 5632
NFC = 44
EPS = 1e-6
NDS = 40
QSC = float(np.log(128.0 ** -0.5))


class R:
    __slots__ = ("w", "rd")

    def __init__(self):
        self.w = None
        self.rd = {}


class TK:
    def __init__(self, nc, stack):
        self.nc = nc
        self.E = {"pe": nc.tensor, "act": nc.scalar, "dve": nc.vector, "pool": nc.gpsimd, "sp": nc.sync}
        self.stack = stack
        self.sem = {}
        self.cnt = {}
        self.seen = {e: {} for e in self.E}
        self.nsem = 0
        for e in self.E:
            self.new_sem(e)
        self.dsems = [stack.enter_context(nc.semaphore(f"dq{i}")) for i in range(NDS)]
        self.dcnt = [0] * NDS
        self.di = 0

    def new_sem(self, e):
        self.nsem += 1
        self.sem[e] = self.stack.enter_context(self.nc.semaphore(f"e{e}{self.nsem}"))
        self.cnt[e] = 0

    def _wait(self, e, s, v):
        if self.seen[e].get(s, 0) < v:
            self.E[e].wait_ge(s, v)
            self.seen[e][s] = v

    def _deps(self, e, reads, writes):
        best = {}
        for r in reads:
            if r.w is not None:
                s, v = r.w
                if best.get(s, 0) < v:
                    best[s] = v
        for w in writes:
            if w.w is not None:
                s, v = w.w
                if best.get(s, 0) < v:
                    best[s] = v
            for s, v in w.rd.items():
                if best.get(s, 0) < v:
                    best[s] = v
        for s, v in best.items():
            if e == "pe" and s is self.sem["pe"]:
                continue
            self._wait(e, s, v)

    def _commit(self, tok, reads, writes):
        s, v = tok
        for r in reads:
            if r.rd.get(s, 0) < v:
                r.rd[s] = v
        for w in writes:
            w.w = tok
            w.rd = {}

    def op(self, e, fn, reads=(), writes=()):
        self._deps(e, reads, writes)
        ins = fn(self.E[e])
        self.cnt[e] += 1
        ins.then_inc(self.sem[e], 1)
        self._commit((self.sem[e], self.cnt[e]), reads, writes)

    def dma(self, q, out, in_, reads=(), writes=()):
        self._deps(q, reads, writes)
        i = self.di
        self.di = (self.di + 1) % NDS
        s = self.dsems[i]
        if self.dcnt[i] > 0:
            self._wait(q, s, self.dcnt[i])
        self.E[q].dma_start(out=out, in_=in_).then_inc(s, 16)
        self.dcnt[i] += 16
        self._commit((s, self.dcnt[i]), reads, writes)

    def barrier(self, engines=("pe", "act", "dve", "sp")):
        for e in engines:
            for o in self.E:
                if o != e and self.cnt[o] > 0:
                    self._wait(e, self.sem[o], self.cnt[o])
            for i in range(NDS):
                if self.dcnt[i] > 0:
                    self._wait(e, self.dsems[i], self.dcnt[i])

    def final_wait(self):
        self.barrier(engines=("sp",))


class Ring:
    def __init__(self, tk, tile, nslot, sched):
        self.tk = tk
        self.tile = tile
        self.nslot = nslot
        self.sched = sched
        self.res = [R() for _ in range(nslot)]
        self.issued = 0

    def get(self, j):
        last = min(j + self.nslot - 1, len(self.sched) - 1)
        while self.issued <= last:
            i = self.issued
            ap, k = self.sched[i]
            s = i % self.nslot
            self.tk.dma("pool", self.tile[:, s, 0:k, :], ap, writes=[self.res[s]])
            self.issued += 1
        s = j % self.nslot
        return self.tile[:, s], self.res[s]

    def slot(self, j):
        assert j < self.issued
        s = j % self.nslot
        return self.tile[:, s], self.res[s]


def build(debug=False):
    nc = bass.Bass("TRN2", target_bir_lowering=False)

    def din(name, shape, dt=F32):
        return nc.dram_tensor(name, list(shape), dt, kind="ExternalInput").ap()

    x = din("x", [4096, D])
    ctx = din("ctx", [256, D])
    cT = din("cT", [128, 16, 2])
    ada_w = din("ada_w", [D, 6 * D])
    ada_b = din("ada_b", [1, 6 * D])
    pre_mix_g = din("pre_mix_g", [1, D])
    post_mix_g = din("post_mix_g", [1, D])
    pre_ffn_g = din("pre_ffn_g", [1, D])
    post_ffn_g = din("post_ffn_g", [1, D])
    w_in = din("w_in", [D, N_IN])
    sgu_ln_g = din("sgu_ln_g", [1, 1024])
    sgu_ln_b = din("sgu_ln_b", [1, 1024])
    sgu_w = din("sgu_w", [4, 128, 128])
    sgu_bT = din("sgu_bT", [128, 4])
    gate_w = din("gate_w", [2, 16, 512])
    gate_b = din("gate_b", [2, 1, 512])
    gla_norm_g = din("gla_norm_g", [1, 256])
    w_out = din("w_out", [D, D])
    w_up = din("w_up", [D, 2 * DFF])
    conv_w = din("conv_w", [128, NFC, 9])
    conv_b = din("conv_b", [128, NFC])
    w_down = din("w_down", [DFF, D])
    c_ident = din("c_ident", [128, 128], BF16)
    c_tri = din("c_tri", [128, 4, 128], BF16)
    c_mask = din("c_mask", [128, 2, 512], F32)
    c_col = din("c_col", [128, 1], BF16)
    c_ones = din("c_ones", [1, 128], BF16)
    out = nc.dram_tensor("out", [2048, D], F32, kind="ExternalOutput").ap()

    modrow = nc.dram_tensor("modrow", [2, 6 * D], F32).ap()
    Pscr = nc.dram_tensor("Pscr", [NT_OWN, 128, 5120], BF16).ap()
    SBscr = nc.dram_tensor("SBscr", [NT_OWN, 128, 1024], BF16).ap()
    H1scr = nc.dram_tensor("H1scr", [16, 128, D], F32).ap()
    Hscr = nc.dram_tensor("Hscr", [NFC, 128, 2048], BF16).ap()

    st = ExitStack()
    with st:
        tk = TK(nc, st)

        def sb(name, shape, dt, stack=st):
            return stack.enter_context(nc.sbuf_tensor(name, list(shape), dt))

        ps = st.enter_context(nc.psum_tensor("ps", [128, 8, 512], F32))
        PB = [R() for _ in range(8)]

        def psb(j):
            return ps[:, j, :]

        def psbf(j):
            return ps[:, j, :].bitcast(BF16)

        def chunk(w, c0, rows0=0, k=16, width=512):
            return (w[rows0:rows0 + 128 * k, c0:c0 + width].rearrange("(k p) n -> p k n", p=128), k)

        sched = []
        for ch in range(24):
            sched.append(chunk(ada_w, ch * 512))
        RI_ADA = 0
        RI_KV = len(sched)
        for c0 in (2048 + 512, 3072, 3072 + 512):
            sched.append(chunk(w_in, c0))
        RI_WIN = len(sched)
        WIN_CH = [0, 1, 2, 3, 4, 8, 9]
        for c in WIN_CH:
            sched.append(chunk(w_in, c * 512))
        RI_WOUT = len(sched)
        for c in range(4):
            sched.append(chunk(w_out, c * 512))
        RI_WUP = len(sched)
        for g in range(11):
            sched.append(chunk(w_up, g * 512))
            sched.append(chunk(w_up, DFF + g * 512))
        RI_WDN = len(sched)
        for dc in range(4):
            for (r0, k) in ((0, 16), (16, 16), (32, 12)):
                sched.append(chunk(w_down, dc * 512, rows0=r0 * 128, k=k))
        ring_t = sb("ring", [128, 4, 16, 512], BF16)
        ring = Ring(tk, ring_t, 4, sched)

        ident = sb("ident", [128, 128], BF16)
        tri = sb("tri", [128, 4, 128], BF16)
        mask = sb("mask", [128, 2, 512], F32)
        col16 = sb("col16", [128, 1], BF16)
        ones = sb("ones", [1, 128], BF16)
        wlr = sb("wlr", [128, 16, 32], BF16)
        wg = sb("wg", [16, 2, 512], BF16)
        bg = sb("bg", [1, 2, 512], BF16)
        CONST = R()
        tk.dma("sp", ident[:], c_ident[:, :], writes=[CONST])
        tk.dma("sp", tri[:], c_tri[:, :, :], writes=[CONST])
        tk.dma("sp", mask[:], c_mask[:, :, :], writes=[CONST])
        tk.dma("sp", col16[:], c_col[:, :], writes=[CONST])
        tk.dma("sp", ones[:], c_ones[:, :], writes=[CONST])
        tk.dma("pool", wlr[:], w_in[:, 5120:5152].rearrange("(k p) n -> p k n", p=128), writes=[CONST])
        tk.dma("pool", wg[:], gate_w.rearrange("t j n -> j t n"), writes=[CONST])
        tk.dma("pool", bg[:], gate_b.rearrange("t o n -> o t n"), writes=[CONST])

        S_f = sb("S_f", [128, 1024], F32)
        S_fb = sb("S_fb", [128, 1024], BF16)
        RS_f = R()
        RS_fb = R()
        RlrT = R()
        junk = sb("junk", [128, 512], BF16)
        big_stack = ExitStack()
        BIG = sb("BIG", [128, 16, NT_OWN * 128], BF16, big_stack)
        RBIG = [R() for _ in range(NT_OWN)]

        def vec_bcast(dst, src_vec, reads=(), writes=()):
            tk.dma("sp", dst, src_vec.partition_broadcast(128), reads=reads, writes=writes)

        MOD = [R() for _ in range(24)]
        PRES = [[R() for _ in range(10)] for _ in range(NT_OWN)]

        with ExitStack() as pst:
            scf = sb("scf", [128, 16, 2], F32, pst)
            scb = sb("scb", [128, 16, 2], BF16, pst)
            modb = sb("modb", [2, 2, 512], F32, pst)
            mods = sb("mods", [2, 2, 512], F32, pst)
            Rscf, Rscb = R(), R()
            Rmodb = [R(), R()]
            Rmods = [R(), R()]
            tk.dma("sp", scf[:], cT[:, :, :], writes=[Rscf])
            tk.op("act", lambda e: e.activation(out=scb[:], in_=scf[:], func=AF.Silu), reads=[Rscf], writes=[Rscb])
            for ch in range(24):
                slot, rres = ring.get(RI_ADA + ch)
                b = ch % 4
                i2 = ch % 2
                tk.dma("sp", modb[:, i2, :], ada_b[0, ch * 512:(ch + 1) * 512].partition_broadcast(2), writes=[Rmodb[i2]])

                def mm(e, slot=slot, b=b):
                    for kc in range(16):
                        ins = e.matmul(ps[0:2, b, :], lhsT=scb[:, kc, :], rhs=slot[:, kc, :], start=(kc == 0), stop=(kc == 15))
                    return ins
                tk.op("pe", mm, reads=[Rscb, rres], writes=[PB[b]])
                tk.op("dve", lambda e, b=b, i2=i2: e.tensor_tensor(out=mods[:, i2, :], in0=ps[0:2, b, :], in1=modb[:, i2, :], op=ALU.add),
                      reads=[PB[b], Rmodb[i2]], writes=[Rmods[i2]])
                tk.dma("sp", modrow[:, ch * 512:(ch + 1) * 512], mods[:, i2, :], reads=[Rmods[i2]], writes=[MOD[ch]])
        tk.barrier()

        def mod_reads(c):
            return MOD[4 * c:4 * c + 4]

        with ExitStack() as pst:
            G1 = sb("G1", [128, D], F32, pst)
            SH1 = sb("SH1", [128, D], F32, pst)
            RG1, RSH1 = R(), R()

            def load_mod(row):
                vec_bcast(G1[:], modrow[row, 2048:4096], reads=mod_reads(1), writes=[RG1])
                vec_bcast(SH1[:], pre_mix_g[0, :], writes=[RSH1])
                tk.op("dve", lambda e: e.scalar_tensor_tensor(out=G1[:], in0=G1[:], scalar=1.0, in1=SH1[:], op0=ALU.add, op1=ALU.mult),
                      reads=[RSH1], writes=[RG1])
                vec_bcast(SH1[:], modrow[row, 0:2048], reads=mod_reads(0), writes=[RSH1])

            xt = sb("xt", [128, 2, D], F32, pst)
            Rxt = [R(), R()]
            ab = sb("ab", [128, 1, D], BF16, pst)
            Rab = [R(), R()]
            Rab[1] = Rab[0]
            aT = sb("aT", [128, 2, 16, 128], BF16, pst)
            RaT = [R(), R()]
            st1 = sb("st1", [128, 2, 4], F32, pst)
            Rst1 = [R(), R()]
            lrs = sb("lrs", [16, 128], BF16, pst)
            Rlrs = R()
            lz = sb("lz", [128, 512], BF16, pst)
            Rlz = R()
            expE = sb("expE", [128, 512], F32, pst)
            RexpE = R()
            ez, Rez = expE, RexpE
            decv = sb("decv", [128, 4], F32, pst)
            Rdecv = R()
            kd = sb("kd", [128, 512], BF16, pst)
            Rkd = R()
            kst = sb("kst", [128, 1, 512], BF16, pst)
            Rkst = [R(), R()]
            Rkst[1] = Rkst[0]
            vb = sb("vb", [128, 1, 1024], BF16, pst)
            Rvb = [R(), R()]
            Rvb[1] = Rvb[0]
            S_b = sb("S_b", [128, 1024], F32, pst)
            RS_b = R()
            S_bb = sb("S_bb", [128, 1, 1024], BF16, pst)
            RS_bb = [R()]
            ring_kv = [ring.get(RI_KV)] + [ring.slot(RI_KV + i) for i in (1, 2)]
            tk.op("dve", lambda e: e.memset(S_f[:], 0.0), writes=[RS_f])
            tk.op("dve", lambda e: e.memset(S_b[:], 0.0), writes=[RS_b])

            tiles = []
            tiles.append((ctx[0:128, :], True, 0, None, True))
            tiles.append((ctx[128:256, :], True, 0, None, True))
            tiles.append((ctx[128:256, :], True, 1, None, True))
            tiles.append((ctx[0:128, :], True, 1, None, True))
            for i in range(NT_ALL - 1, -1, -1):
                tiles.append((x[i * 128:(i + 1) * 128, :], False, 1, i if i < NT_OWN else None, i > 0))
            NTL = len(tiles)

            def load_x(n):
                if n < NTL:
                    tk.dma("sp", xt[:, n % 2, :], tiles[n][0], writes=[Rxt[n % 2]])

            def dst_of(n):
                own = tiles[n][3]
                if own is not None:
                    return BIG[:, :, own * 128:(own + 1) * 128], RBIG[own]
                return aT[:, n % 2, :, :], RaT[n % 2]

            def front(n):
                p = n % 2
                if n == 0:
                    load_mod(1)
                if n == 4:
                    load_mod(0)
                xs = xt[:, p, :]
                tk.op("act", lambda e: e.activation(out=ab[:, 0, :], in_=xs, func=AF.Square, accum_out=st1[:, p, 0:1]), reads=[Rxt[p]], writes=[Rst1[p], Rab[p]])
                tk.op("act", lambda e: e.activation(out=st1[:, p, 1:2], in_=st1[:, p, 0:1], func=AF.Ln, scale=1.0 / D, bias=EPS), reads=[], writes=[Rst1[p]])
                tk.op("act", lambda e: e.activation(out=st1[:, p, 2:3], in_=st1[:, p, 1:2], func=AF.Exp, scale=-0.5), reads=[], writes=[Rst1[p]])
                tk.op("dve", lambda e: e.scalar_tensor_tensor(out=xs, in0=xs, scalar=st1[:, p, 2:3], in1=G1[:], op0=ALU.mult, op1=ALU.mult),
                      reads=[Rst1[p], RG1], writes=[Rxt[p]])
                tk.op("dve", lambda e: e.tensor_tensor(out=ab[:, 0, :], in0=xs, in1=SH1[:], op=ALU.add), reads=[Rxt[p], RSH1], writes=[Rab[p]])

            def trans(n):
                p = n % 2
                dstT, RdT = dst_of(n)
                for hf in range(2):
                    def tr(e, hf=hf):
                        for j in range(8):
                            kc = hf * 8 + j
                            ins = e.transpose(out=psbf(hf)[:, j * 128:(j + 1) * 128], in_=ab[:, 0, kc * 128:(kc + 1) * 128], identity=ident[:])
                        return ins
                    tk.op("pe", tr, reads=[Rab[p], CONST], writes=[PB[hf]])
                    tk.op("act", lambda e, hf=hf: e.activation(out=dstT[:, hf * 8:(hf + 1) * 8, :], in_=psbf(hf).rearrange("p (j t) -> p j t", t=128), func=AF.Identity),
                          reads=[PB[hf]], writes=[RdT])

            def proj_a(n):
                src, is_ctx, dr, own, upd = tiles[n]
                p = n % 2
                dstT, RdT = dst_of(n)
                for bi, (slot, rres) in enumerate(ring_kv):
                    def mm(e, slot=slot, bi=bi):
                        for kc in range(16):
                            ins = e.matmul(psb(2 + bi), lhsT=dstT[:, kc, :], rhs=slot[:, kc, :], start=(kc == 0), stop=(kc == 15))
                        return ins
                    tk.op("pe", mm, reads=[RdT, rres], writes=[PB[2 + bi]])
                if upd:
                    def mml(e):
                        for kc in range(16):
                            ins = e.matmul(ps[0:16, 6, 0:128], lhsT=wlr[:, kc, dr * 16:(dr + 1) * 16], rhs=dstT[:, kc, :], start=(kc == 0), stop=(kc == 15))
                        return ins
                    tk.op("pe", mml, reads=[RdT, CONST], writes=[PB[6]])
                    tk.op("act", lambda e: e.activation(out=lrs[:], in_=ps[0:16, 6, 0:128], func=AF.Identity), reads=[PB[6]], writes=[Rlrs])
                tk.op("act", lambda e: e.activation(out=vb[:, 0, :], in_=ps[:, 3:5, :].rearrange("p a n -> p (a n)"), func=AF.Identity), reads=[PB[3], PB[4]], writes=[Rvb[p]])
                if own is not None:
                    tk.op("act", lambda e: e.activation(out=kst[:, 0, :], in_=psb(2), func=AF.Identity), reads=[PB[2]], writes=[Rkst[p]])
                    tk.dma("sp", Pscr[own, :, 2560:3072], kst[:, 0, :], reads=[Rkst[p]], writes=[PRES[own][5]])
                    tk.dma("sp", Pscr[own, :, 3072:4096], vb[:, 0, :], reads=[Rvb[p]], writes=[PRES[own][6]])

            def proj_b(n):
                src, is_ctx, dr, own, upd = tiles[n]
                p = n % 2
                Sx, RSx = (S_f, RS_f) if dr == 0 else (S_b, RS_b)

                def mmz(e):
                    e.matmul(psb(5), lhsT=lrs[:], rhs=wg[:, dr, :], start=True, stop=False)
                    return e.matmul(psb(5), lhsT=ones[:], rhs=bg[:, dr, :], start=False, stop=True)
                tk.op("pe", mmz, reads=[Rlrs, CONST], writes=[PB[5]])
                tk.op("act", lambda e: e.activation(out=ez[:], in_=psb(5), func=AF.Exp, scale=-1.0), reads=[PB[5]], writes=[Rez])
                tk.op("act", lambda e: e.activation(out=lz[:], in_=ez[:], func=AF.Ln, bias=1.0), reads=[Rez], writes=[Rlz])
                tidx = 1 if dr == 0 else 3
                tk.op("pe", lambda e: e.matmul(psb(5), lhsT=tri[:, tidx, :], rhs=lz[:], start=True, stop=True), reads=[Rlz, CONST], writes=[PB[5]])

                def mmd(e):
                    for h in range(4):
                        ins = e.matmul(ps[:, 6, 128 + h:129 + h], lhsT=lz[:, h * 128:(h + 1) * 128], rhs=col16[:], start=True, stop=True)
                    return ins
                tk.op("pe", mmd, reads=[Rlz, CONST], writes=[PB[6]])
                tk.op("act", lambda e: e.activation(out=expE[:], in_=psb(5), func=AF.Exp), reads=[PB[5]], writes=[RexpE])
                tk.op("act", lambda e: e.activation(out=decv[:], in_=ps[:, 6, 128:132], func=AF.Exp), reads=[PB[6]], writes=[Rdecv])
                tk.op("dve", lambda e: e.tensor_tensor(out=kd[:], in0=psb(2), in1=expE[:], op=ALU.mult), reads=[PB[2], RexpE], writes=[Rkd])
                for hp in range(2):
                    def mmkv(e, hp=hp):
                        for hh in range(2):
                            h = hp * 2 + hh
                            ins = e.matmul(ps[:, 7, hh * 256:(hh + 1) * 256], lhsT=kd[:, h * 128:(h + 1) * 128], rhs=vb[:, 0, h * 256:(h + 1) * 256], start=True, stop=True)
                        return ins
                    tk.op("pe", mmkv, reads=[Rkd, Rvb[p]], writes=[PB[7]])
                    for hh in range(2):
                        h = hp * 2 + hh
                        tk.op("dve", lambda e, h=h, hh=hh: e.scalar_tensor_tensor(out=Sx[:, h * 256:(h + 1) * 256], in0=Sx[:, h * 256:(h + 1) * 256], scalar=decv[:, h:h + 1],
                                                                             in1=ps[:, 7, hh * 256:(hh + 1) * 256], op0=ALU.mult, op1=ALU.add),
                              reads=[PB[7], Rdecv], writes=[RSx])

            load_x(0)
            load_x(1)
            front(0)
            trans(0)
            for n in range(NTL):
                src, is_ctx, dr, own, upd = tiles[n]
                if n + 1 < NTL:
                    front(n + 1)
                load_x(n + 2)
                if own is not None:
                    tk.op("act", lambda e: e.activation(out=S_bb[:, 0, :], in_=S_b[:], func=AF.Identity), reads=[RS_b], writes=[RS_bb[0]])
                    tk.dma("sp", SBscr[own], S_bb[:, 0, :], reads=[RS_bb[0]], writes=[R()])
                proj_a(n)
                if n + 1 < NTL:
                    trans(n + 1)
                if upd:
                    proj_b(n)
            tk.op("act", lambda e: e.activation(out=S_fb[:], in_=S_f[:], func=AF.Identity), reads=[RS_f], writes=[RS_fb])
        tk.barrier()
        lrT = sb("lrT", [16, 2, NT_OWN * 128], BF16, big_stack)

        if debug:
            dbg_aT = nc.dram_tensor("dbg_aT", [128, 16, NT_OWN * 128], BF16, kind="ExternalOutput").ap()
            dbg_Sf = nc.dram_tensor("dbg_Sf", [128, 1024], F32, kind="ExternalOutput").ap()
            tk.dma("sp", dbg_aT[:, :, :], BIG[:], reads=RBIG)
            tk.dma("sp", dbg_Sf[:, :], S_f[:], reads=[RS_f])

        with ExitStack() as pst:
            stg = sb("stgB", [128, 4, 512], BF16, pst)
            Rstg = [R() for _ in range(4)]
            funcs = [AF.Gelu_apprx_tanh] * 4 + [AF.Copy] * 4 + [AF.Silu] * 2
            it = 0
            for ci, c in enumerate(WIN_CH):
                slot, rres = ring.get(RI_WIN + ci)
                for t in range(NT_OWN):
                    b = it % 4
                    s4 = it % 4
                    it += 1

                    def mm(e, slot=slot, b=b, t=t):
                        for kc in range(16):
                            ins = e.matmul(psb(b), lhsT=BIG[:, kc, t * 128:(t + 1) * 128], rhs=slot[:, kc, :], start=(kc == 0), stop=(kc == 15))
                        return ins
                    tk.op("pe", mm, reads=[RBIG[t], rres], writes=[PB[b]])
                    tk.op("act", lambda e, b=b, s4=s4, c=c: e.activation(out=stg[:, s4, :], in_=psb(b), func=funcs[c]), reads=[PB[b]], writes=[Rstg[s4]])
                    tk.dma("sp", Pscr[t, :, c * 512:(c + 1) * 512], stg[:, s4, :], reads=[Rstg[s4]], writes=[PRES[t][c]])
            for dr in range(2):
                for tb in range(5):
                    n0 = tb * 512
                    nn = min(512, NT_OWN * 128 - n0)
                    b = 4 + (dr * 5 + tb) % 4

                    def mml(e, dr=dr, n0=n0, nn=nn, b=b):
                        for kc in range(16):
                            ins = e.matmul(ps[0:16, b, 0:nn], lhsT=wlr[:, kc, dr * 16:(dr + 1) * 16], rhs=BIG[:, kc, n0:n0 + nn], start=(kc == 0), stop=(kc == 15))
                        return ins
                    tk.op("pe", mml, reads=RBIG + [CONST], writes=[PB[b]])
                    tk.op("act", lambda e, dr=dr, n0=n0, nn=nn, b=b: e.activation(out=lrT[:, dr, n0:n0 + nn], in_=ps[0:16, b, 0:nn], func=AF.Identity), reads=[PB[b]], writes=[RlrT])
        tk.barrier()

        with ExitStack() as pst:
            lng = sb("lng", [128, 1024], F32, pst)
            Bc = sb("Bc", [128, 1024], F32, pst)
            ngb = sb("ngb", [128, 256], F32, pst)
            wsT = sb("wsT", [128, 4, 128], BF16, pst)
            w1 = sb("w1", [128, 4], F32, pst)
            bsT = sb("bsT", [128, 4], F32, pst)
            Rlng, RBc, Rngb, Rwsf, Rwsb, RwsT, Rw1, RbsT = [R() for _ in range(8)]
            vec_bcast(lng[:], sgu_ln_g[0, :], writes=[Rlng])
            vec_bcast(Bc[:], sgu_ln_b[0, :], writes=[RBc])
            vec_bcast(ngb[:], gla_norm_g[0, :], writes=[Rngb])
            tk.dma("sp", bsT[:], sgu_bT[:, :], writes=[RbsT])
            with ExitStack() as sst:
                wsf = sb("wsf", [128, 4, 128], F32, sst)
                wsb = sb("wsb", [128, 4, 128], BF16, sst)
                tk.dma("sp", wsf[:], sgu_w.rearrange("g p q -> p g q"), writes=[Rwsf])
                tk.op("dve", lambda e: e.tensor_copy(out=wsb[:], in_=wsf[:]), reads=[Rwsf], writes=[Rwsb])
                tk.op("dve", lambda e: e.tensor_reduce(out=w1[:], in_=wsf[:], axis=AX.X, op=ALU.add), reads=[Rwsf], writes=[Rw1])

                def trw(e):
                    for g in range(4):
                        ins = e.transpose(out=psbf(0)[:, g * 128:(g + 1) * 128], in_=wsb[:, g, :], identity=ident[:])
                    return ins
                tk.op("pe", trw, reads=[Rwsb, CONST], writes=[PB[0]])
                tk.op("act", lambda e: e.activation(out=wsT[:], in_=psbf(0)[:, 0:512].rearrange("p (g t) -> p g t", t=128), func=AF.Identity), reads=[PB[0]], writes=[RwsT])
                tk.barrier()
            for g in range(4):
                tk.op("dve", lambda e, g=g: e.tensor_scalar(out=Bc[:, g * 256:(g + 1) * 256], in0=Bc[:, g * 256:(g + 1) * 256], scalar1=w1[:, g:g + 1], scalar2=bsT[:, g:g + 1],
                                                            op0=ALU.mult, op1=ALU.add), reads=[Rw1, RbsT], writes=[RBc])

            Pt = sb("Pt", [128, 1, 5120], BF16, pst)
            RPtA, RPtB = R(), R()
            Sbt = sb("Sbt", [128, 1, 1024], BF16, pst)
            RSbt = [R()]
            stat = sb("stat", [128, 4, 6], F32, pst)
            mv = sb("mv", [128, 4, 2], F32, pst)
            rs4 = sb("rs4", [128, 8], F32, pst)
            Rstat, Rmv, Rrs4 = R(), R(), R()
            vh = sb("vh", [128, 1024], BF16, pst)
            Rvh = R()
            sy = sb("sy", [128, 1024], F32, pst)
            Rsy = R()
            GS, RGS = sy, Rsy
            yb = sb("yb", [128, D], BF16, pst)
            Ryb = R()
            ex = sb("ex", [128, 2, 512], F32, pst)
            Rex = [R(), R()]
            ez2, Rez2 = ex, Rex
            lz2 = sb("lz2", [128, 2, 512], BF16, pst)
            Rlz2 = [R(), R()]
            qk = sb("qk", [128, 4, 512], BF16, pst)
            Rqk = [R() for _ in range(4)]
            qkT = sb("qkT", [128, 4, 512], BF16, pst)
            RqkT = [R() for _ in range(4)]
            kdf = sb("kdf", [128, 512], BF16, pst)
            Rkdf = R()
            decf = sb("decf", [128, 4], F32, pst)
            Rdecf = R()
            att = sb("att", [128, 2, 512], BF16, pst)
            Ratt = [R(), R()]
            osq = sb("osq", [128, 8], F32, pst)
            Rosq = R()

            def load_pa(t):
                if t < NT_OWN:
                    tk.dma("sp", Pt[:, 0, 0:2048], Pscr[t, :, 0:2048], reads=PRES[t][0:4], writes=[RPtA])

            def load_pb(t):
                if t < NT_OWN:
                    tk.dma("sp", Pt[:, 0, 2048:5120], Pscr[t, :, 2048:5120], reads=PRES[t][4:10], writes=[RPtB])
                    tk.dma("sp", Sbt[:, 0, :], SBscr[t], writes=[RSbt[0]])
            load_pa(0)
            load_pb(0)
            for t in range(NT_OWN):
                P = Pt[:, 0, :]
                rPA, rPB = RPtA, RPtB
                Sb_t = Sbt[:, 0, :]
                rSb = RSbt[0]
                gu, gv, qv, kv_, vv, rr = P[:, 0:1024], P[:, 1024:2048], P[:, 2048:2560], P[:, 2560:3072], P[:, 3072:4096], P[:, 4096:5120]
                for g in range(4):
                    tk.op("dve", lambda e, g=g: e.bn_stats(out=stat[:, g, :], in_=gv[:, g * 256:(g + 1) * 256]), reads=[rPA], writes=[Rstat])
                    tk.op("dve", lambda e, g=g: e.bn_aggr(out=mv[:, g, :], in_=stat[:, g, :]), reads=[Rstat], writes=[Rmv])
                tk.op("act", lambda e: e.activation(out=rs4[:, 0:4], in_=mv[:, :, 1], func=AF.Ln, bias=EPS), reads=[Rmv], writes=[Rrs4])
                tk.op("act", lambda e: e.activation(out=rs4[:, 4:8], in_=rs4[:, 0:4], func=AF.Exp, scale=-0.5), reads=[Rrs4], writes=[Rrs4])
                for g in range(4):
                    tk.op("dve", lambda e, g=g: e.tensor_scalar(out=vh[:, g * 256:(g + 1) * 256], in0=gv[:, g * 256:(g + 1) * 256], scalar1=mv[:, g, 0:1], scalar2=rs4[:, 4 + g:5 + g],
                                                                op0=ALU.subtract, op1=ALU.mult), reads=[rPA, Rmv, Rrs4], writes=[Rvh])

                def mms(e):
                    for g in range(4):
                        ins = e.matmul(ps[:, g // 2, (g % 2) * 256:(g % 2 + 1) * 256], lhsT=wsT[:, g, :], rhs=vh[:, g * 256:(g + 1) * 256], start=True, stop=True)
                    return ins
                tk.op("pe", mms, reads=[Rvh, RwsT], writes=[PB[0], PB[1]])
                tk.op("dve", lambda e: e.tensor_tensor(out=sy[:], in0=ps[:, 0:2, :].rearrange("p a n -> p (a n)"), in1=lng[:], op=ALU.mult), reads=[PB[0], PB[1], Rlng], writes=[Rsy])
                tk.op("dve", lambda e: e.tensor_tensor(out=sy[:], in0=sy[:], in1=Bc[:], op=ALU.add), reads=[RBc], writes=[Rsy])
                tk.op("dve", lambda e, gu=gu: e.tensor_tensor(out=yb[:, 0:1024], in0=sy[:], in1=gu, op=ALU.mult), reads=[Rsy, rPA], writes=[Ryb])
                load_pa(t + 1)
                for dr in range(2):
                    def mmz(e, dr=dr):
                        e.matmul(psb(2 + dr), lhsT=lrT[:, dr, t * 128:(t + 1) * 128], rhs=wg[:, dr, :], start=True, stop=False)
                        return e.matmul(psb(2 + dr), lhsT=ones[:], rhs=bg[:, dr, :], start=False, stop=True)
                    tk.op("pe", mmz, reads=[RlrT, CONST], writes=[PB[2 + dr]])
                    tk.op("act", lambda e, dr=dr: e.activation(out=ez2[:, dr, :], in_=psb(2 + dr), func=AF.Exp, scale=-1.0), reads=[PB[2 + dr]], writes=[Rez2[dr]])
                    tk.op("act", lambda e, dr=dr: e.activation(out=lz2[:, dr, :], in_=ez2[:, dr, :], func=AF.Ln, bias=1.0), reads=[Rez2[dr]], writes=[Rlz2[dr]])
                tk.op("pe", lambda e: e.matmul(psb(2), lhsT=tri[:, 0, :], rhs=lz2[:, 0, :], start=True, stop=True), reads=[Rlz2[0], CONST], writes=[PB[2]])
                tk.op("pe", lambda e: e.matmul(psb(3), lhsT=tri[:, 2, :], rhs=lz2[:, 1, :], start=True, stop=True), reads=[Rlz2[1], CONST], writes=[PB[3]])
                tk.op("pe", lambda e: e.matmul(psb(4), lhsT=tri[:, 1, :], rhs=lz2[:, 0, :], start=True, stop=True), reads=[Rlz2[0], CONST], writes=[PB[4]])

                def mmd(e):
                    for h in range(4):
                        ins = e.matmul(ps[:, 5, h:h + 1], lhsT=lz2[:, 0, h * 128:(h + 1) * 128], rhs=col16[:], start=True, stop=True)
                    return ins
                tk.op("pe", mmd, reads=[Rlz2[0], CONST], writes=[PB[5]])
                tk.op("act", lambda e: e.activation(out=decf[:], in_=ps[:, 5, 0:4], func=AF.Exp), reads=[PB[5]], writes=[Rdecf])
                for dr in range(2):
                    bk = 2 + dr
                    tk.op("act", lambda e, bk=bk: e.activation(out=ex[:, 0, :], in_=psb(bk), func=AF.Exp, bias=QSC), reads=[PB[bk]], writes=[Rex[0]])
                    tk.op("dve", lambda e, dr=dr: e.tensor_tensor(out=qk[:, 2 * dr, :], in0=ex[:, 0, :], in1=qv, op=ALU.mult), reads=[Rex[0], rPB], writes=[Rqk[2 * dr]])
                    tk.op("act", lambda e, bk=bk: e.activation(out=ex[:, 1, :], in_=psb(bk), func=AF.Exp, scale=-1.0), reads=[PB[bk]], writes=[Rex[1]])
                    tk.op("dve", lambda e, dr=dr: e.tensor_tensor(out=qk[:, 2 * dr + 1, :], in0=ex[:, 1, :], in1=kv_, op=ALU.mult), reads=[Rex[1], rPB], writes=[Rqk[2 * dr + 1]])
                tk.op("act", lambda e: e.activation(out=ex[:, 0, :], in_=psb(4), func=AF.Exp), reads=[PB[4]], writes=[Rex[0]])
                tk.op("dve", lambda e: e.tensor_tensor(out=kdf[:], in0=ex[:, 0, :], in1=kv_, op=ALU.mult), reads=[Rex[0], rPB], writes=[Rkdf])
                for a in range(4):
                    bk = 2 + a // 2
                    off = (a % 2) * 512

                    def trq(e, a=a, bk=bk, off=off):
                        for h in range(4):
                            ins = e.transpose(out=psbf(bk)[:, off + h * 128: off + (h + 1) * 128], in_=qk[:, a, h * 128:(h + 1) * 128], identity=ident[:])
                        return ins
                    tk.op("pe", trq, reads=[Rqk[a], CONST], writes=[PB[bk]])
                for a in range(4):
                    bk = 2 + a // 2
                    off = (a % 2) * 512
                    tk.op("act", lambda e, a=a, bk=bk, off=off: e.activation(out=qkT[:, a, :], in_=psbf(bk)[:, off:off + 512], func=AF.Identity), reads=[PB[bk]], writes=[RqkT[a]])
                for dr in range(2):
                    bk = 4 + dr

                    def mma(e, dr=dr, bk=bk):
                        for h in range(4):
                            ins = e.matmul(ps[:, bk, h * 128:(h + 1) * 128], lhsT=qkT[:, 2 * dr + 1, h * 128:(h + 1) * 128], rhs=qkT[:, 2 * dr, h * 128:(h + 1) * 128], start=True, stop=True)
                        return ins
                    tk.op("pe", mma, reads=[RqkT[2 * dr], RqkT[2 * dr + 1]], writes=[PB[bk]])
                    tk.op("dve", lambda e, dr=dr, bk=bk: e.tensor_tensor(out=att[:, dr, :].rearrange("p (h t) -> p h t", t=128), in0=ps[:, bk, :].rearrange("p (h t) -> p h t", t=128),
                                                                       in1=mask[:, dr, :].rearrange("p (h t) -> p h t", t=128), op=ALU.mult), reads=[PB[bk], CONST], writes=[Ratt[dr]])
                def mmo(e):
                    for h in range(4):
                        o_ap = ps[:, h // 2, (h % 2) * 256:(h % 2 + 1) * 256]
                        e.matmul(o_ap, lhsT=att[:, 0, h * 128:(h + 1) * 128], rhs=vv[:, h * 256:(h + 1) * 256], start=True, stop=False)
                        e.matmul(o_ap, lhsT=att[:, 1, h * 128:(h + 1) * 128], rhs=vv[:, h * 256:(h + 1) * 256], start=False, stop=False)
                        e.matmul(o_ap, lhsT=qkT[:, 0, h * 128:(h + 1) * 128], rhs=S_fb[:, h * 256:(h + 1) * 256], start=False, stop=False)
                        ins = e.matmul(o_ap, lhsT=qkT[:, 2, h * 128:(h + 1) * 128], rhs=Sb_t[:, h * 256:(h + 1) * 256], start=False, stop=True)
                    return ins
                tk.op("pe", mmo, reads=[Ratt[0], Ratt[1], rPB, RqkT[0], RqkT[2], RS_fb, rSb], writes=[PB[0], PB[1]])
                for hp in range(2):
                    def mmkv(e, hp=hp):
                        for hh in range(2):
                            h = hp * 2 + hh
                            ins = e.matmul(ps[:, 6 + hp, hh * 256:(hh + 1) * 256], lhsT=kdf[:, h * 128:(h + 1) * 128], rhs=vv[:, h * 256:(h + 1) * 256], start=True, stop=True)
                        return ins
                    tk.op("pe", mmkv, reads=[Rkdf, rPB], writes=[PB[6 + hp]])
                for h in range(4):
                    tk.op("dve", lambda e, h=h: e.scalar_tensor_tensor(out=S_f[:, h * 256:(h + 1) * 256], in0=S_f[:, h * 256:(h + 1) * 256], scalar=decf[:, h:h + 1],
                                                                       in1=ps[:, 6 + h // 2, (h % 2) * 256:(h % 2 + 1) * 256], op0=ALU.mult, op1=ALU.add),
                          reads=[PB[6 + h // 2], Rdecf], writes=[RS_f])
                tk.op("act", lambda e: e.activation(out=S_fb[:], in_=S_f[:], func=AF.Identity), reads=[RS_f], writes=[RS_fb])
                for h in range(4):
                    tk.op("act", lambda e, h=h: e.activation(out=junk[:, 0:256], in_=ps[:, h // 2, (h % 2) * 256:(h % 2 + 1) * 256], func=AF.Square, accum_out=osq[:, h:h + 1]),
                          reads=[PB[h // 2]], writes=[Rosq])
                tk.op("act", lambda e: e.activation(out=osq[:, 4:8], in_=osq[:, 0:4], func=AF.Ln, scale=1.0 / 256, bias=EPS), reads=[Rosq], writes=[Rosq])
                tk.op("act", lambda e: e.activation(out=osq[:, 0:4], in_=osq[:, 4:8], func=AF.Exp, scale=-0.5), reads=[Rosq], writes=[Rosq])
                for h in range(4):
                    tk.op("dve", lambda e, h=h: e.tensor_tensor(out=GS[:, h * 256:(h + 1) * 256], in0=rr[:, h * 256:(h + 1) * 256], in1=ngb[:], op=ALU.mult),
                          reads=[rPB, Rngb], writes=[RGS])
                for h in range(4):
                    tk.op("dve", lambda e, h=h: e.scalar_tensor_tensor(out=yb[:, 1024 + h * 256:1024 + (h + 1) * 256], in0=ps[:, h // 2, (h % 2) * 256:(h % 2 + 1) * 256], scalar=osq[:, h:h + 1],
                                                                       in1=GS[:, h * 256:(h + 1) * 256], op0=ALU.mult, op1=ALU.mult), reads=[PB[h // 2], Rosq, RGS], writes=[Ryb])
                for hf in range(2):
                    def tr(e, hf=hf):
                        for j in range(8):
                            kc = hf * 8 + j
                            ins = e.transpose(out=psbf(2 + hf)[:, j * 128:(j + 1) * 128], in_=yb[:, kc * 128:(kc + 1) * 128], identity=ident[:])
                        return ins
                    tk.op("pe", tr, reads=[Ryb, CONST], writes=[PB[2 + hf]])
                    tk.op("act", lambda e, hf=hf: e.activation(out=BIG[:, hf * 8:(hf + 1) * 8, t * 128:(t + 1) * 128], in_=psbf(2 + hf).rearrange("p (j t) -> p j t", t=128), func=AF.Identity),
                          reads=[PB[2 + hf]], writes=[RBIG[t]])
                load_pb(t + 1)
        tk.barrier()

        if debug:
            dbg_yT = nc.dram_tensor("dbg_yT", [128, 16, NT_OWN * 128], BF16, kind="ExternalOutput").ap()
            tk.dma("sp", dbg_yT[:, :, :], BIG[:], reads=RBIG)

        H1RES = [R() for _ in range(16)]
        with ExitStack() as pst:
            GT1 = sb("GT1", [128, D], F32, pst)
            G2 = sb("G2", [128, D], F32, pst)
            SH2 = sb("SH2", [128, D], F32, pst)
            RGT1, RG2, RSH2 = R(), R(), R()
            vec_bcast(GT1[:], modrow[0, 4096:6144], reads=mod_reads(2), writes=[RGT1])
            vec_bcast(G2[:], post_mix_g[0, :], writes=[RG2])
            tk.op("dve", lambda e: e.tensor_tensor(out=GT1[:], in0=GT1[:], in1=G2[:], op=ALU.mult), reads=[RG2], writes=[RGT1])
            vec_bcast(G2[:], modrow[0, 8192:10240], reads=mod_reads(4), writes=[RG2])
            vec_bcast(SH2[:], pre_ffn_g[0, :], writes=[RSH2])
            tk.op("dve", lambda e: e.scalar_tensor_tensor(out=G2[:], in0=G2[:], scalar=1.0, in1=SH2[:], op0=ALU.add, op1=ALU.mult), reads=[RSH2], writes=[RG2])
            vec_bcast(SH2[:], modrow[0, 6144:8192], reads=mod_reads(3), writes=[RSH2])
            xt = sb("xtD", [128, 1, D], F32, pst)
            Rxt = [R()]
            h1 = sb("h1", [128, 1, D], F32, pst)
            Rh1 = [R()]
            fb = sb("fb", [128, D], BF16, pst)
            Rfb = R()
            sq = sb("sqD", [128, 16], F32, pst)
            Rsq = R()
            wo = [ring.get(RI_WOUT)] + [ring.slot(RI_WOUT + c) for c in (1, 2, 3)]

            def load_x(t):
                if t < NT_OWN:
                    tk.dma("sp", xt[:, 0, :], x[t * 128:(t + 1) * 128, :], writes=[Rxt[0]])

            def mm_d(t):
                st_ = (t % 2) * 4
                for c in range(4):
                    slot, rres = wo[c]

                    def mm(e, slot=slot, c=c):
                        for kc in range(16):
                            ins = e.matmul(psb(st_ + c), lhsT=BIG[:, kc, t * 128:(t + 1) * 128], rhs=slot[:, kc, :], start=(kc == 0), stop=(kc == 15))
                        return ins
                    tk.op("pe", mm, reads=[RBIG[t], rres], writes=[PB[st_ + c]])

            def post_d(t):
                st_ = (t % 2) * 4
                xs = xt[:, 0, :]
                rx = Rxt[0]
                hs = h1[:, 0, :]
                rh = Rh1[0]
                for c in range(4):
                    tk.op("act", lambda e, c=c: e.activation(out=junk[:, 0:512], in_=psb(st_ + c), func=AF.Square, accum_out=sq[:, c:c + 1]), reads=[PB[st_ + c]], writes=[Rsq])
                tk.op("dve", lambda e: e.tensor_reduce(out=sq[:, 4:5], in_=sq[:, 0:4], axis=AX.X, op=ALU.add), reads=[Rsq], writes=[Rsq])
                tk.op("act", lambda e: e.activation(out=sq[:, 5:6], in_=sq[:, 4:5], func=AF.Ln, scale=1.0 / D, bias=EPS), reads=[Rsq], writes=[Rsq])
                tk.op("act", lambda e: e.activation(out=sq[:, 6:7], in_=sq[:, 5:6], func=AF.Exp, scale=-0.5), reads=[Rsq], writes=[Rsq])
                tk.op("dve", lambda e: e.scalar_tensor_tensor(out=hs, in0=ps[:, st_:st_ + 4, :].rearrange("p a n -> p (a n)"), scalar=sq[:, 6:7], in1=GT1[:], op0=ALU.mult, op1=ALU.mult),
                      reads=[PB[st_], PB[st_ + 1], PB[st_ + 2], PB[st_ + 3], Rsq, RGT1], writes=[rh])
                tk.op("dve", lambda e: e.tensor_tensor(out=hs, in0=hs, in1=xs, op=ALU.add), reads=[rx], writes=[rh])
                if t < 16:
                    tk.dma("sp", H1scr[t], hs, reads=[rh], writes=[H1RES[t]])
                tk.op("act", lambda e: e.activation(out=fb[:], in_=hs, func=AF.Square, accum_out=sq[:, 8:9]), reads=[rh], writes=[Rsq, Rfb])
                tk.op("act", lambda e: e.activation(out=sq[:, 9:10], in_=sq[:, 8:9], func=AF.Ln, scale=1.0 / D, bias=EPS), reads=[Rsq], writes=[Rsq])
                tk.op("act", lambda e: e.activation(out=sq[:, 10:11], in_=sq[:, 9:10], func=AF.Exp, scale=-0.5), reads=[Rsq], writes=[Rsq])
                tk.op("dve", lambda e: e.scalar_tensor_tensor(out=xs, in0=hs, scalar=sq[:, 10:11], in1=G2[:], op0=ALU.mult, op1=ALU.mult), reads=[rh, Rsq, RG2], writes=[rx])
                tk.op("dve", lambda e: e.tensor_tensor(out=fb[:], in0=xs, in1=SH2[:], op=ALU.add), reads=[rx, RSH2], writes=[Rfb])
                for hf in range(2):
                    def tr(e, hf=hf):
                        for j in range(8):
                            kc = hf * 8 + j
                            ins = e.transpose(out=psbf(st_ + hf)[:, j * 128:(j + 1) * 128], in_=fb[:, kc * 128:(kc + 1) * 128], identity=ident[:])
                        return ins
                    tk.op("pe", tr, reads=[Rfb, CONST], writes=[PB[st_ + hf]])
                    tk.op("act", lambda e, hf=hf: e.activation(out=BIG[:, hf * 8:(hf + 1) * 8, t * 128:(t + 1) * 128], in_=psbf(st_ + hf).rearrange("p (j t) -> p j t", t=128), func=AF.Identity),
                          reads=[PB[st_ + hf]], writes=[RBIG[t]])

            mm_d(0)
            for t in range(NT_OWN):
                if t + 1 < NT_OWN:
                    mm_d(t + 1)
                load_x(t)
                post_d(t)
        tk.barrier()

        HRES = [R() for _ in range(NFC)]
        with ExitStack() as pst:
            cw = sb("cw", [128, NFC, 9], F32, pst)
            cb = sb("cb", [128, NFC], F32, pst)
            Rcw = R()
            tk.dma("sp", cw[:], conv_w[:, :, :], writes=[Rcw])
            tk.dma("sp", cb[:], conv_b[:, :], writes=[Rcw])
            gts = sb("gts", [128, 2, 33 * 64], F32, pst)
            Rgts = [R(), R()]
            acc = sb("acc", [128, 2048], F32, pst)
            Racc = R()
            hb = sb("hb", [128, 2, 2048], BF16, pst)
            Rhb = [R(), R()]
            nblk = [(0, 512), (512, 512), (1024, 512), (1536, 512), (2048, 64)]
            gbank = 0
            for fc in range(NFC):
                g, sub = fc // 4, fc % 4
                sg, rg = ring.get(RI_WUP + 2 * g)
                sv, rv = ring.slot(RI_WUP + 2 * g + 1)
                gt_ = gts[:, fc % 2, :]
                rgt = Rgts[fc % 2]
                for (n0, nn) in nblk:
                    b = gbank % 4
                    gbank += 1

                    def mm(e, b=b, n0=n0, nn=nn, sg=sg, sub=sub):
                        for kc in range(16):
                            ins = e.matmul(ps[:, b, 0:nn], lhsT=sg[:, kc, sub * 128:(sub + 1) * 128], rhs=BIG[:, kc, n0:n0 + nn], start=(kc == 0), stop=(kc == 15))
                        return ins
                    tk.op("pe", mm, reads=RBIG + [rg], writes=[PB[b]])
                    tk.op("act", lambda e, b=b, n0=n0, nn=nn, gt_=gt_: e.activation(out=gt_[:, n0:n0 + nn], in_=ps[:, b, 0:nn], func=AF.Identity), reads=[PB[b]], writes=[rgt])
                for tb in range(4):
                    def mmv(e, tb=tb, sv=sv, sub=sub):
                        for kc in range(16):
                            ins = e.matmul(psb(4 + tb), lhsT=sv[:, kc, sub * 128:(sub + 1) * 128], rhs=BIG[:, kc, tb * 512:(tb + 1) * 512], start=(kc == 0), stop=(kc == 15))
                        return ins
                    tk.op("pe", mmv, reads=RBIG[0:16] + [rv], writes=[PB[4 + tb]])
                gv3 = gt_.rearrange("p (r c) -> p r c", c=64)
                av3 = acc[:].rearrange("p (r c) -> p r c", c=64)
                tk.op("dve", lambda e, gt_=gt_, fc=fc: e.tensor_scalar(out=acc[:], in0=gt_[:, 0:2048], scalar1=cw[:, fc, 4:5], scalar2=None, op0=ALU.mult), reads=[rgt, Rcw], writes=[Racc])
                for i in range(3):
                    for j in range(3):
                        if i == 1 and j == 1:
                            continue
                        dr_, dc_ = i - 1, j - 1
                        r0 = 1 if dr_ == -1 else 0
                        c0 = max(0, -dc_)
                        c1 = min(64, 64 - dc_)
                        tk.op("dve", lambda e, gv3=gv3, fc=fc, i=i, j=j, r0=r0, c0=c0, c1=c1, dr_=dr_, dc_=dc_: e.scalar_tensor_tensor(
                            out=av3[:, r0:32, c0:c1], in0=gv3[:, r0 + dr_:32 + dr_, c0 + dc_:c1 + dc_], scalar=cw[:, fc, 3 * i + j:3 * i + j + 1], in1=av3[:, r0:32, c0:c1],
                            op0=ALU.mult, op1=ALU.add), reads=[rgt, Rcw], writes=[Racc])
                tk.op("act", lambda e, fc=fc: e.activation(out=acc[:], in_=acc[:], func=AF.Gelu_apprx_tanh, bias=cb[:, fc:fc + 1]), reads=[Rcw], writes=[Racc])
                hs = hb[:, fc % 2, :]
                rh = Rhb[fc % 2]
                tk.op("dve", lambda e, hs=hs: e.tensor_tensor(out=hs, in0=ps[:, 4:8, :].rearrange("p a n -> p (a n)"), in1=acc[:], op=ALU.mult), reads=[PB[4], PB[5], PB[6], PB[7], Racc], writes=[rh])
                tk.dma("sp", Hscr[fc], hs, reads=[rh], writes=[HRES[fc]])
        tk.barrier()
        big_stack.close()

        with ExitStack() as pst:
            ssq = sb("ssq", [128, 16, 4], F32, pst)
            Rssq = R()
            YRES = [[R() for _ in range(4)] for _ in range(16)]
            f1 = ExitStack()
            Hg = sb("Hg", [128, 2, NFC, 512], BF16, f1)
            RHg = [R(), R()]
            ys = sb("ys", [128, 4, 512], F32, f1)
            Rys = [R() for _ in range(4)]
            Hv = Hscr.rearrange("f p t -> p f t")
            seq = [(dc, tg) for dc in range(4) for tg in range(4)]

            def load_h(n):
                if n < len(seq):
                    dc, tg = seq[n]
                    tk.dma("sp", Hg[:, n % 2, :, :], Hv[:, :, tg * 512:(tg + 1) * 512], reads=HRES, writes=[RHg[n % 2]])
            load_h(0)
            it = 0
            for n, (dc, tg) in enumerate(seq):
                load_h(n + 1)
                slots = [ring.get(RI_WDN + dc * 3)] + [ring.slot(RI_WDN + dc * 3 + i) for i in (1, 2)]
                for tl in range(4):
                    t = tg * 4 + tl
                    b = it % 8
                    s4 = it % 4
                    it += 1

                    def mm(e, b=b, tl=tl, n=n, slots=slots):
                        for fc in range(NFC):
                            slot = slots[fc // 16][0]
                            ins = e.matmul(psb(b), lhsT=Hg[:, n % 2, fc, tl * 128:(tl + 1) * 128], rhs=slot[:, fc % 16, :], start=(fc == 0), stop=(fc == NFC - 1))
                        return ins
                    tk.op("pe", mm, reads=[RHg[n % 2]] + [s[1] for s in slots], writes=[PB[b]])
                    tk.op("act", lambda e, b=b, t=t, dc=dc: e.activation(out=junk[:, 0:512], in_=psb(b), func=AF.Square, accum_out=ssq[:, t, dc:dc + 1]), reads=[PB[b]], writes=[Rssq])
                    tk.op("act", lambda e, b=b, s4=s4: e.activation(out=ys[:, s4, :], in_=psb(b), func=AF.Identity), reads=[PB[b]], writes=[Rys[s4]])
                    tk.dma("sp", out[t * 128:(t + 1) * 128, dc * 512:(dc + 1) * 512], ys[:, s4, :], reads=[Rys[s4]], writes=[YRES[t][dc]])
            tk.barrier()
            f1.close()
            GT2 = sb("GT2", [128, D], F32, pst)
            yt = sb("ytF", [128, 2, D], F32, pst)
            Ryt = [R(), R()]
            ht = sb("htF", [128, 2, D], F32, pst)
            Rht = [R(), R()]
            RGT2 = R()
            vec_bcast(GT2[:], modrow[0, 10240:12288], reads=mod_reads(5), writes=[RGT2])
            vec_bcast(yt[:, 1, :], post_ffn_g[0, :], writes=[Ryt[1]])
            tk.op("dve", lambda e: e.tensor_tensor(out=GT2[:], in0=GT2[:], in1=yt[:, 1, :], op=ALU.mult), reads=[Ryt[1]], writes=[RGT2])
            rs = sb("rsF", [128, 4], F32, pst)
            Rrs = R()
            OUTRES = R()

            def load_f(t):
                if t < 16:
                    tk.dma("sp", yt[:, t % 2, :], out[t * 128:(t + 1) * 128, :], reads=YRES[t], writes=[Ryt[t % 2]])
                    tk.dma("sp", ht[:, t % 2, :], H1scr[t], reads=[H1RES[t]], writes=[Rht[t % 2]])
            load_f(0)
            for t in range(16):
                load_f(t + 1)
                ya = yt[:, t % 2, :]
                ha = ht[:, t % 2, :]
                tk.op("dve", lambda e, t=t: e.tensor_reduce(out=rs[:, 0:1], in_=ssq[:, t, :], axis=AX.X, op=ALU.add), reads=[Rssq], writes=[Rrs])
                tk.op("act", lambda e: e.activation(out=rs[:, 1:2], in_=rs[:, 0:1], func=AF.Ln, scale=1.0 / D, bias=EPS), reads=[Rrs], writes=[Rrs])
                tk.op("act", lambda e: e.activation(out=rs[:, 2:3], in_=rs[:, 1:2], func=AF.Exp, scale=-0.5), reads=[Rrs], writes=[Rrs])
                tk.op("dve", lambda e, ya=ya: e.scalar_tensor_tensor(out=ya, in0=ya, scalar=rs[:, 2:3], in1=GT2[:], op0=ALU.mult, op1=ALU.mult), reads=[Rrs, RGT2], writes=[Ryt[t % 2]])
                tk.op("dve", lambda e, ya=ya, ha=ha: e.tensor_tensor(out=ya, in0=ya, in1=ha, op=ALU.add), reads=[Rht[t % 2]], writes=[Ryt[t % 2]])
                tk.dma("sp", out[t * 128:(t + 1) * 128, :], ya, reads=[Ryt[t % 2]], writes=[R()])
        tk.final_wait()
    return nc


_NC_CACHE = {}


def _consts():
    s = np.arange(128)[:, None]
    t = np.arange(128)[None, :]
    m = -1.0 / 16.0
    tri = np.stack([(s <= t) * m, (s > t) * m, (s >= t) * m, (s < t) * m], axis=1).astype(ml_dtypes.bfloat16)
    mask = np.stack([np.tile((s <= t) * 1.0, (1, 4)), np.tile((s >= t) * 1.0, (1, 4))], axis=1).astype(np.float32)
    return dict(
        c_ident=np.eye(128, dtype=ml_dtypes.bfloat16),
        c_tri=np.ascontiguousarray(tri),
        c_mask=np.ascontiguousarray(mask),
        c_col=np.full((128, 1), m, dtype=ml_dtypes.bfloat16),
        c_ones=np.ones((1, 128), dtype=ml_dtypes.bfloat16),
    )


def _prep_core(b, half, I, shared):
    flip = half == 1
    f32 = lambda a: np.ascontiguousarray(a, dtype=np.float32)
    xb = I["x"][b]
    cx = I["ctx"][b]
    if flip:
        xb = xb[::-1]
        cx = cx[::-1]
    cc = np.stack([I["c"][b], I["c_ctx"]], axis=0)
    cT = cc.reshape(2, 16, 128).transpose(2, 1, 0)
    m = dict(shared[flip])
    m.update(x=f32(xb), ctx=f32(cx), cT=f32(cT))
    return m


def _prep_shared(I):
    f32 = lambda a: np.ascontiguousarray(a, dtype=np.float32)
    res = {}
    base = dict(
        ada_w=f32(I["ada_w"][0]), ada_b=f32(I["ada_b"][0][None, :]),
        pre_mix_g=f32(I["pre_mix_g"][0][None]), post_mix_g=f32(I["post_mix_g"][0][None]),
        pre_ffn_g=f32(I["pre_ffn_g"][0][None]), post_ffn_g=f32(I["post_ffn_g"][0][None]),
        sgu_ln_g=f32(I["sgu_ln_g"][0][None]), sgu_ln_b=f32(I["sgu_ln_b"][0][None]),
        gla_norm_g=f32(I["gla_norm_g"][0][None]),
        w_out=f32(I["w_out"][0]), w_up=f32(I["ffn_w_up"][0]), w_down=f32(I["ffn_w_down"][0]),
        conv_b=f32(I["ffn_conv_b"][0].reshape(NFC, 128).T),
    )
    base.update(_consts())
    for flip in (False, True):
        m = dict(base)
        w_in = I["w_in"][0]
        sw = I["sgu_w"][0]
        sbias = I["sgu_b"][0]
        gw = np.stack([I["gla_gate_w_f"][0], I["gla_gate_w_b"][0]], axis=0)
        gb = np.stack([I["gla_gate_b_f"][0], I["gla_gate_b_b"][0]], axis=0)[:, None, :]
        cw = I["ffn_conv_w"][0]
        if flip:
            w_in = np.concatenate([w_in[:, :5120], w_in[:, 5136:5152], w_in[:, 5120:5136]], axis=1)
            sw = sw[:, ::-1, ::-1]
            sbias = sbias[:, ::-1]
            gw = gw[::-1]
            gb = gb[::-1]
            cw = cw[::-1, ::-1, :]
        m["w_in"] = f32(w_in)
        m["sgu_w"] = f32(sw)
        m["sgu_bT"] = f32(sbias.T)
        m["gate_w"] = f32(gw)
        m["gate_b"] = f32(gb)
        m["conv_w"] = f32(cw.reshape(9, NFC, 128).transpose(2, 1, 0))
        res[flip] = m
    return res


def kernel(**inputs):
    I = {k: np.asarray(v) for k, v in inputs.items()}
    if "nc" not in _NC_CACHE:
        _NC_CACHE["nc"] = build()
    nc = _NC_CACHE["nc"]
    shared = _prep_shared(I)
    in_maps = []
    for core in range(8):
        b, half = core // 2, core % 2
        in_maps.append(_prep_core(b, half, I, shared))
    res = run_bass_kernel_spmd(nc, in_maps, core_ids=list(range(8)))
    outp = np.empty((4, 4096, D), dtype=np.float32)
    for core in range(8):
        b, half = core // 2, core % 2
        o = np.asarray(res.results[core]["out"])
        if half == 0:
            outp[b, 0:2048] = o
        else:
            outp[b, 2048:4096] = o[::-1]
    return outp
```
